# Optimizing a Trainium2 kernel written in Bass

```python
import math
import jax, jax.numpy as jnp
from jax import lax
import numpy as np

D_MODEL = 1024
BATCH = 1
SEQ = 16384
DEPTH = 2
DEC_BATCH = 32
DEC_SEQ = 32
PAST_LEN = 2048

CHUNK = 64
SSD_EXPAND = 2
D_SSM = SSD_EXPAND * D_MODEL
SSD_HEAD_DIM = 64
SSD_HEADS = D_SSM // SSD_HEAD_DIM
SSD_GROUPS = 4
HEADS_PER_GROUP = SSD_HEADS // SSD_GROUPS
SSD_STATE = 128
SSD_CONV = 4
SSD_CONV_DIM = D_SSM + 2 * SSD_GROUPS * SSD_STATE
SB_HEAD_DIM = 128
SB_HEADS = D_MODEL // SB_HEAD_DIM
SB_WIDTH = SB_HEADS * SB_HEAD_DIM
SB_BLOCK = 128
D_FF = 4 * D_MODEL
FFN_CONV = 3
PLE_DIM = 256
IN_DIM = D_SSM + SSD_CONV_DIM + SSD_HEADS + 3 * SB_WIDTH + 2 * D_MODEL
EPS = 1e-6

kernel_name = 'hybrid_ssd_stickbreak_convffn_stream_step'


def rmsnorm(x, gain):
    xf = x.astype(jnp.float32)
    xf = xf * lax.rsqrt(jnp.mean(xf * xf, axis=-1, keepdims=True) + EPS)
    return (xf * gain.astype(jnp.float32)).astype(x.dtype)


def causal_dwconv(xp, w, b):
    width = w.shape[0]
    length = xp.shape[1] - (width - 1)
    out = b
    for tap in range(width):
        out = out + xp[:, tap:tap + length] * w[tap]
    return out


def gated_group_rmsnorm(y, z, w):
    g = (y * jax.nn.silu(z)).astype(jnp.float32)
    shp = g.shape
    g = g.reshape(shp[:-1] + (SSD_GROUPS, D_SSM // SSD_GROUPS))
    g = g * lax.rsqrt(jnp.mean(g * g, axis=-1, keepdims=True) + EPS)
    return (g.reshape(shp) * w.astype(jnp.float32)).astype(y.dtype)


def ssd_chunked_scan(xs, dt, a, bm, cm, h0):
    f32 = jnp.float32
    bsz, seqlen = xs.shape[:2]
    cl = CHUNK if seqlen >= CHUNK else seqlen
    nc = seqlen // cl
    xg = xs.astype(f32).reshape(bsz, nc, cl, SSD_GROUPS, HEADS_PER_GROUP, SSD_HEAD_DIM)
    dtg = dt.astype(f32).reshape(bsz, nc, cl, SSD_GROUPS, HEADS_PER_GROUP)
    ag = dtg * a.astype(f32).reshape(SSD_GROUPS, HEADS_PER_GROUP)
    bg = bm.astype(f32).reshape(bsz, nc, cl, SSD_GROUPS, SSD_STATE)
    cg = cm.astype(f32).reshape(bsz, nc, cl, SSD_GROUPS, SSD_STATE)
    a_cum = jnp.cumsum(ag, axis=2)
    x_dt = xg * dtg[..., None]
    causal = jnp.tril(jnp.ones((cl, cl), dtype=bool))[None, None, :, :, None, None]
    seg = a_cum[:, :, :, None] - a_cum[:, :, None, :]
    decay_ls = jnp.exp(jnp.where(causal, seg, -jnp.inf))
    cb = jnp.einsum('bclgn,bcsgn->bclsg', cg, bg)
    y_diag = jnp.einsum('bclsge,bcsgep->bclgep', cb[..., None] * decay_ls, x_dt)
    decay_to_end = jnp.exp(a_cum[:, :, -1:] - a_cum)
    chunk_states = jnp.einsum('bclgn,bclgep->bcgepn', bg, x_dt * decay_to_end[..., None])
    chunk_decay = jnp.exp(a_cum[:, :, -1])

    def step(h, inp):
        dec, st = inp
        return dec[..., None, None] * h + st, h

    h_init = h0.astype(f32).reshape(bsz, SSD_GROUPS, HEADS_PER_GROUP, SSD_HEAD_DIM, SSD_STATE)
    h_last, h_prev = lax.scan(step, h_init,
                              (jnp.moveaxis(chunk_decay, 1, 0), jnp.moveaxis(chunk_states, 1, 0)))
    h_prev = jnp.moveaxis(h_prev, 0, 1)
    y_off = jnp.einsum('bclgn,bcgepn->bclgep', cg, h_prev) * jnp.exp(a_cum)[..., None]
    y = (y_diag + y_off).reshape(bsz, seqlen, SSD_HEADS, SSD_HEAD_DIM)
    return y, h_last.reshape(bsz, SSD_HEADS, SSD_HEAD_DIM, SSD_STATE)


def stick_breaking(q, k, v, q_start):
    f32 = jnp.float32
    bsz, lq = q.shape[:2]
    lk = k.shape[1]
    nkb_total = -(-lk // SB_BLOCK)
    pad = nkb_total * SB_BLOCK - lk
    k = jnp.pad(k, ((0, 0), (0, pad), (0, 0), (0, 0)))
    v = jnp.pad(v, ((0, 0), (0, pad), (0, 0), (0, 0)))
    blk = SB_BLOCK if lq % SB_BLOCK == 0 else lq
    nb = lq // blk
    scale = SB_HEAD_DIM ** -0.5
    idx = jnp.arange(SB_BLOCK)
    tri_in = (idx[:, None] > idx[None, :]).astype(f32)
    outs = []
    for bi in range(nb):
        q_lo = q_start + bi * blk
        nkb = max(1, min(nkb_total, -(-(q_lo + blk - 1) // SB_BLOCK)))
        kk = k[:, :nkb * SB_BLOCK]
        vv = v[:, :nkb * SB_BLOCK]
        q_blk = q[:, bi * blk:(bi + 1) * blk]
        z = jnp.einsum('bqhd,bkhd->bhqk', q_blk, kk).astype(f32) * scale
        z = z.reshape(bsz, SB_HEADS, blk, nkb, SB_BLOCK)
        q_pos = q_lo + jnp.arange(blk)
        k_pos = jnp.arange(nkb * SB_BLOCK).reshape(nkb, SB_BLOCK)
        earlier = k_pos[None, None, None] < q_pos[None, None, :, None, None]
        log_keep = jnp.where(earlier, jax.nn.log_sigmoid(-z), 0.0)
        within = jnp.einsum('bhqcj,js->bhqcs', log_keep, tri_in)
        totals = jnp.sum(log_keep, axis=-1)
        cidx = jnp.arange(nkb)
        tri_blk = (cidx[:, None] > cidx[None, :]).astype(f32)
        later = jnp.einsum('bhqd,dc->bhqc', totals, tri_blk)
        weights = jnp.where(earlier,
                            jnp.exp(jax.nn.log_sigmoid(z) + within + later[..., None]), 0.0)
        weights = weights.reshape(bsz, SB_HEADS, blk, nkb * SB_BLOCK)
        outs.append(jnp.einsum('bhqk,bkhd->bqhd', weights.astype(v.dtype), vv))
    o = jnp.concatenate(outs, axis=1) if nb > 1 else outs[0]
    return o.reshape(bsz, lq, SB_WIDTH)


def trunk_layer(x, p_i, conv_st, ssm_st, k_past, v_past, ffn_st, lw):
    bsz, length, _ = x.shape
    h = rmsnorm(x, lw['norm_pre_mix'])
    proj = h @ lw['w_in']
    widths = (D_SSM, SSD_CONV_DIM, SSD_HEADS, SB_WIDTH, SB_WIDTH, SB_WIDTH, 2 * D_MODEL)
    offsets = [int(o) for o in np.cumsum(widths)[:-1]]
    z, xbc, dt_raw, q, k, v, gate_logits = jnp.split(proj, offsets, axis=-1)
    xbc_pad = jnp.concatenate([conv_st, xbc], axis=1)
    new_conv = xbc_pad[:, -(SSD_CONV - 1):]
    xbc_c = jax.nn.silu(causal_dwconv(xbc_pad, lw['ssd_conv_w'], lw['ssd_conv_b']))
    xs, bm, cm = jnp.split(xbc_c, [D_SSM, D_SSM + SSD_GROUPS * SSD_STATE], axis=-1)
    dt = jax.nn.softplus(dt_raw.astype(jnp.float32) + lw['ssd_dt_bias'].astype(jnp.float32))
    a = -jnp.exp(lw['ssd_a_log'].astype(jnp.float32))
    xs_h = xs.reshape(bsz, length, SSD_HEADS, SSD_HEAD_DIM)
    y, new_ssm = ssd_chunked_scan(xs_h, dt, a,
                                  bm.reshape(bsz, length, SSD_GROUPS, SSD_STATE),
                                  cm.reshape(bsz, length, SSD_GROUPS, SSD_STATE), ssm_st)
    y = (y + lw['ssd_d'].astype(jnp.float32)[:, None] * xs_h.astype(jnp.float32)).astype(x.dtype)
    y = gated_group_rmsnorm(y.reshape(bsz, length, D_SSM), z, lw['ssd_norm'])
    branch_ssd = y @ lw['w_br_ssd']
    q = q.reshape(bsz, length, SB_HEADS, SB_HEAD_DIM)
    k_new = k.reshape(bsz, length, SB_HEADS, SB_HEAD_DIM)
    v_new = v.reshape(bsz, length, SB_HEADS, SB_HEAD_DIM)
    k_all = jnp.concatenate([k_past, k_new], axis=1)
    v_all = jnp.concatenate([v_past, v_new], axis=1)
    branch_sb = stick_breaking(q, k_all, v_all, k_past.shape[1]) @ lw['w_br_sb']
    g_ssd, g_sb = jnp.split(jax.nn.sigmoid(gate_logits), 2, axis=-1)
    mixed = (g_ssd * branch_ssd + g_sb * branch_sb) @ lw['w_out']
    x = x + rmsnorm(mixed, lw['norm_post_mix'])
    h2 = rmsnorm(x, lw['norm_pre_ffn'])
    ff_gate, ff_up = jnp.split(h2 @ lw['w_up'], 2, axis=-1)
    gate_pad = jnp.concatenate([ffn_st, ff_gate], axis=1)
    new_ffn = gate_pad[:, -(FFN_CONV - 1):]
    ff = jax.nn.gelu(causal_dwconv(gate_pad, lw['ffn_conv_w'], lw['ffn_conv_b']), approximate=True) * ff_up
    x = x + rmsnorm(ff @ lw['w_down'], lw['norm_post_ffn'])
    ple = (p_i @ lw['w_ple']) * jax.nn.sigmoid(x @ lw['w_ple_gate'])
    x = x + rmsnorm(ple, lw['norm_ple'])
    return x, (new_conv, new_ssm.astype(ssm_st.dtype), k_new, v_new, new_ffn)


def setup_inputs(seed: int = 0) -> dict:
    key = jax.random.key(seed)
    ks = iter(jax.random.split(key, 40))

    def nrm(shape, scale=1.0):
        return jax.random.normal(next(ks), shape, jnp.float32) * scale

    def gain(shape):
        return 1.0 + nrm(shape, 0.05)

    dt_u = jax.random.uniform(next(ks), (DEPTH, SSD_HEADS), jnp.float32)
    dt0 = jnp.exp(dt_u * (math.log(0.1) - math.log(0.001)) + math.log(0.001))
    dt_bias = dt0 + jnp.log(-jnp.expm1(-dt0))
    a_log = jnp.log(jax.random.uniform(next(ks), (DEPTH, SSD_HEADS), jnp.float32, minval=1.0, maxval=16.0))
    return {
        'x_prompt': nrm((BATCH, SEQ, D_MODEL)),
        'x_sample': nrm((DEC_BATCH, DEC_SEQ, D_MODEL)),
        'state_ssd_conv': nrm((DEPTH, DEC_BATCH, SSD_CONV - 1, SSD_CONV_DIM)),
        'state_ssd': nrm((DEPTH, DEC_BATCH, SSD_HEADS, SSD_HEAD_DIM, SSD_STATE), 0.1),
        'cache_sb_k': nrm((DEPTH, DEC_BATCH, PAST_LEN, SB_HEADS, SB_HEAD_DIM)),
        'cache_sb_v': nrm((DEPTH, DEC_BATCH, PAST_LEN, SB_HEADS, SB_HEAD_DIM)),
        'state_ffn_conv': nrm((DEPTH, DEC_BATCH, FFN_CONV - 1, D_FF)),
        'p_prompt': nrm((DEPTH, BATCH, SEQ, PLE_DIM)),
        'p_sample': nrm((DEPTH, DEC_BATCH, DEC_SEQ, PLE_DIM)),
        'norm_pre_mix': gain((DEPTH, D_MODEL)),
        'w_in': nrm((DEPTH, D_MODEL, IN_DIM), D_MODEL ** -0.5),
        'ssd_conv_w': nrm((DEPTH, SSD_CONV, SSD_CONV_DIM), SSD_CONV ** -0.5),
        'ssd_conv_b': nrm((DEPTH, SSD_CONV_DIM), 0.02),
        'ssd_dt_bias': dt_bias,
        'ssd_a_log': a_log,
        'ssd_d': gain((DEPTH, SSD_HEADS)),
        'ssd_norm': gain((DEPTH, D_SSM)),
        'w_br_ssd': nrm((DEPTH, D_SSM, D_MODEL), D_SSM ** -0.5),
        'w_br_sb': nrm((DEPTH, SB_WIDTH, D_MODEL), SB_WIDTH ** -0.5),
        'w_out': nrm((DEPTH, D_MODEL, D_MODEL), D_MODEL ** -0.5),
        'norm_post_mix': gain((DEPTH, D_MODEL)),
        'norm_pre_ffn': gain((DEPTH, D_MODEL)),
        'w_up': nrm((DEPTH, D_MODEL, 2 * D_FF), D_MODEL ** -0.5),
        'ffn_conv_w': nrm((DEPTH, FFN_CONV, D_FF), FFN_CONV ** -0.5),
        'ffn_conv_b': nrm((DEPTH, D_FF), 0.02),
        'w_down': nrm((DEPTH, D_FF, D_MODEL), D_FF ** -0.5),
        'norm_post_ffn': gain((DEPTH, D_MODEL)),
        'w_ple': nrm((DEPTH, PLE_DIM, D_MODEL), PLE_DIM ** -0.5),
        'w_ple_gate': nrm((DEPTH, D_MODEL, D_MODEL), D_MODEL ** -0.5),
        'norm_ple': gain((DEPTH, D_MODEL)),
    }


def reference(x_prompt, x_sample, state_ssd_conv, state_ssd, cache_sb_k, cache_sb_v, state_ffn_conv,
              p_prompt, p_sample, norm_pre_mix, w_in, ssd_conv_w, ssd_conv_b, ssd_dt_bias, ssd_a_log,
              ssd_d, ssd_norm, w_br_ssd, w_br_sb, w_out, norm_post_mix, norm_pre_ffn, w_up, ffn_conv_w,
              ffn_conv_b, w_down, norm_post_ffn, w_ple, w_ple_gate, norm_ple):
    def run(x, p, conv0, ssm0, k0, v0, ffn0):
        per_layer = []
        for i in range(DEPTH):
            lw = {
                'norm_pre_mix': norm_pre_mix[i], 'w_in': w_in[i],
                'ssd_conv_w': ssd_conv_w[i], 'ssd_conv_b': ssd_conv_b[i],
                'ssd_dt_bias': ssd_dt_bias[i], 'ssd_a_log': ssd_a_log[i], 'ssd_d': ssd_d[i],
                'ssd_norm': ssd_norm[i], 'w_br_ssd': w_br_ssd[i], 'w_br_sb': w_br_sb[i],
                'w_out': w_out[i], 'norm_post_mix': norm_post_mix[i],
                'norm_pre_ffn': norm_pre_ffn[i], 'w_up': w_up[i],
                'ffn_conv_w': ffn_conv_w[i], 'ffn_conv_b': ffn_conv_b[i],
                'w_down': w_down[i], 'norm_post_ffn': norm_post_ffn[i],
                'w_ple': w_ple[i], 'w_ple_gate': w_ple_gate[i], 'norm_ple': norm_ple[i],
            }
            x, st = trunk_layer(x, p[i], conv0[i], ssm0[i], k0[i], v0[i], ffn0[i], lw)
            per_layer.append(st)
        stacked = [jnp.stack([st[j] for st in per_layer]) for j in range(5)]
        return x, stacked

    bp = x_prompt.shape[0]
    dtp = x_prompt.dtype
    zero_conv = jnp.zeros((DEPTH, bp, SSD_CONV - 1, SSD_CONV_DIM), dtp)
    zero_ssm = jnp.zeros((DEPTH, bp, SSD_HEADS, SSD_HEAD_DIM, SSD_STATE), dtp)
    empty_kv = jnp.zeros((DEPTH, bp, 0, SB_HEADS, SB_HEAD_DIM), dtp)
    zero_ffn = jnp.zeros((DEPTH, bp, FFN_CONV - 1, D_FF), dtp)
    y_prompt, prompt_states = run(x_prompt, p_prompt, zero_conv, zero_ssm, empty_kv, empty_kv, zero_ffn)
    y_sample, sample_states = run(x_sample, p_sample, state_ssd_conv, state_ssd, cache_sb_k, cache_sb_v,
                                  state_ffn_conv)
    prompt_ssd_conv, prompt_ssd_state, prompt_sb_k, prompt_sb_v, prompt_ffn_conv = prompt_states
    sample_ssd_conv, sample_ssd_state, sample_sb_k, sample_sb_v, sample_ffn_conv = sample_states
    return (y_prompt, y_sample,
            prompt_ssd_conv, prompt_ssd_state, prompt_sb_k, prompt_sb_v, prompt_ffn_conv,
            sample_ssd_conv, sample_ssd_state, sample_sb_k, sample_sb_v, sample_ffn_conv)
```

```python
import os
from contextlib import ExitStack
import numpy as np
import concourse.bass as bass
import concourse.mybir as mybir
from concourse.bass_utils import run_bass_kernel_spmd

F32 = mybir.dt.float32
BF16 = mybir.dt.bfloat16
ALU = mybir.AluOpType
AF = mybir.ActivationFunctionType
EPOCH = 24000
SKIPN = int(os.environ.get("MK_SKIPN", "3"))
MAXOPS = int(os.environ.get("MK_MAXOPS", "100000000"))
NC = 8
D = 1024
TOK = 2176
NT = 17408
EPS = 1e-6
STAGE = int(os.environ.get("MK_STAGE", "9"))
SBSEL = os.environ.get("MK_SB", "")
LVL = int(os.environ.get("MK_LVL", "9"))


class Buf:
    __slots__ = ("name", "lw", "rd", "excl")

    def __init__(self, name, excl=False):
        self.name = name
        self.lw = None
        self.rd = []
        self.excl = excl


class Eng:
    def __init__(self, name):
        self.name = name
        self.seq = 0
        self.items = []
        self.waited = {}
        self.dslot = 0
        self.duse = []


class _Rec:
    def __getattr__(self, name):
        return lambda *a, **k: (name, a, k)


class FW:
    NSLOT = {"sp": 4, "pool": 3, "act": 2}

    def __init__(self, nc):
        self.nc = nc
        self.E = {n: Eng(n) for n in ("pe", "act", "dve", "pool", "sp")}
        self.ncust = 0
        self.dkeys = {}

    def _need(self, eng, tok):
        e = self.E[eng]
        if tok[0] == "e":
            _, e2, s = tok
            if e2 == eng and (eng == "pe" or e.seq - s >= SKIPN):
                return
            if e.waited.get(e2, 0) >= s:
                return
            e.waited[e2] = s
            ep = (s - 1) // EPOCH
            e.items.append(("w", ("e", e2, ep), s - ep * EPOCH))
        else:
            _, key, val = tok
            if e.waited.get(key, 0) >= val:
                return
            e.waited[key] = val
            e.items.append(("w", key, val))

    def _deps(self, eng, reads, writes):
        for b in reads:
            if b.lw is not None:
                self._need(eng, b.lw)
            if b.excl:
                for t in b.rd:
                    if not (t[0] == "e" and t[1] == eng):
                        self._need(eng, t)
        for b in writes:
            if b.lw is not None:
                self._need(eng, b.lw)
            for t in b.rd:
                self._need(eng, t)

    def _mark(self, tok, reads, writes):
        for b in writes:
            b.lw = tok
            b.rd = []
        for b in reads:
            b.rd.append(tok)
            if len(b.rd) > 40:
                b.rd = b.rd[-40:]

    def _skip(self):
        self.nrec = getattr(self, "nrec", 0) + 1
        return self.nrec > MAXOPS

    def op(self, eng, fn, reads=(), writes=()):
        if self._skip():
            return
        e = self.E[eng]
        self._deps(eng, reads, writes)
        e.seq += 1
        ep = (e.seq - 1) // EPOCH
        e.items.append(("o", fn(_Rec()), ("e", eng, ep)))
        self._mark(("e", eng, e.seq), reads, writes)

    def dma(self, q, out, in_, reads=(), writes=()):
        if self._skip():
            return
        e = self.E[q]
        self._deps(q, reads, writes)
        if not e.duse:
            e.duse = [0] * self.NSLOT[q]
        s = e.dslot
        e.dslot = (s + 1) % self.NSLOT[q]
        gen = e.duse[s] // 3000
        key = ("d", q, s, gen)
        if e.duse[s] % 3000 > 0:
            self._need(q, ("d", key, 16 * (e.duse[s] % 3000)))
        elif e.duse[s] > 0:
            self._need(q, ("d", ("d", q, s, gen - 1), 16 * 3000))
        e.duse[s] += 1
        val = 16 * ((e.duse[s] - 1) % 3000 + 1)
        self.dkeys[(q, s)] = ("d", key, val)
        e.items.append(("d", (lambda h, o=out, i=in_: h.dma_start(out=o, in_=i)), key))
        self._mark(("d", key, val), reads, writes)

    def custom(self, q, fn, reads=(), writes=(), inc=1):
        e = self.E[q]
        self._deps(q, reads, writes)
        self.ncust += 1
        key = ("c", q, 0)
        e.items.append(("c", fn(_Rec()), key, inc))
        self._mark(("d", key, inc * self.ncust), reads, writes)

    def barrier(self):
        names = list(self.E)
        for a in names:
            for b in names:
                if a != b and self.E[b].seq > 0:
                    self._need(a, ("e", b, self.E[b].seq))
            for tok in self.dkeys.values():
                self._need(a, tok)

    def emit(self, stack):
        nc = self.nc
        self.barrier()
        keys = []
        seen = set()
        for e in self.E.values():
            for it in e.items:
                k = it[1] if it[0] == "w" else it[2]
                if k not in seen:
                    seen.add(k)
                    keys.append(k)
        semh = {}
        for i, k in enumerate(keys):
            semh[k] = stack.enter_context(nc.semaphore("s%d" % i))
        print("FW: sems", len(keys), {n: e.seq for n, e in self.E.items()},
              {n: len(e.items) for n, e in self.E.items()}, flush=True)
        block = stack.enter_context(nc.Block())

        def run(e):
            def body(h):
                for it in e.items:
                    if it[0] == "w":
                        h.wait_ge(semh[it[1]], it[2])
                    elif it[0] == "o":
                        getattr(h, it[1][0])(*it[1][1], **it[1][2]).then_inc(semh[it[2]], 1)
                    elif it[0] == "d":
                        it[1](h).then_inc(semh[it[2]], 16)
                    else:
                        getattr(h, it[1][0])(*it[1][1], **it[1][2]).then_inc(semh[it[2]], it[3])
            return body

        block.tensor(run(self.E["pe"]))
        block.scalar(run(self.E["act"]))
        block.vector(run(self.E["dve"]))
        block.gpsimd(run(self.E["pool"]))
        block.sync(run(self.E["sp"]))


def build():
    nc = bass.Bass("TRN2", target_bir_lowering=False)
    fw = FW(nc)

    def din(name, shape, dt=F32):
        return nc.dram_tensor(name, list(shape), dt, kind="ExternalInput").ap()

    def dout(name, shape, dt=F32):
        return nc.dram_tensor(name, list(shape), dt, kind="ExternalOutput").ap()

    def dscr(name, shape, dt=F32):
        return nc.dram_tensor(name, list(shape), dt).ap()

    xT = din("xT", [D, TOK])
    pT = din("pT", [2, 256, TOK])
    w_fm = din("w_fm", [2, D, 1024])
    w_tm = din("w_tm", [2, D, 132])
    w_gate = din("w_gate", [2, D, 2048])
    w_brssd = din("w_brssd", [2, 2048, D])
    w_brsb = din("w_brsb", [2, D, D])
    w_out = din("w_out", [2, D, D])
    w_up = din("w_up", [2, D, 8192])
    w_down = din("w_down", [2, 4096, D])
    w_ple = din("w_ple", [2, 256, D])
    w_pleg = din("w_pleg", [2, D, D])
    gains = din("gains", [2, 128, 5, 8])
    gssd = din("gssd", [2, 128, 2, 8])
    convw = din("convw", [2, 128, 4, 4])
    convb = din("convb", [2, 128, 4])
    dtb = din("dtb", [2, 128, 4])
    alog = din("alog", [2, 128, 4])
    dskip = din("dskip", [2, 128, 2])
    fcw = din("fcw", [2, 128, 32, 3])
    fcb = din("fcb", [2, 128, 32])
    conv0 = din("conv0", [2, 128, 4, 32, 3])
    h0p = din("h0p", [2, 32, 2, 128, 128])
    h0T = din("h0T", [2, 32, 128, 256])
    kcT = din("kcT", [2, 2, 16, 128, 16, 128])
    vc = din("vc", [2, 2, 16, 128, 16, 128])
    ffn0 = din("ffn0", [2, 128, 32, 4, 2])
    cmask = din("cmask", [128, 10, 128])
    seqm = din("seqm", [128, 4])
    amask = din("amask", [128, 5, 512])
    yT = dout("yT", [D, TOK])
    o_kT = dout("o_kT", [2, 128, NT])
    o_v = dout("o_v", [2, NT, 128])
    o_convP = dout("o_convP", [2, 128, 4, 3])
    o_convS = dout("o_convS", [2, 128, 4, 32, 3])
    o_ssdP = dout("o_ssdP", [2, 128, 256])
    o_ssdS = dout("o_ssdS", [2, 32, 2, 128, 128])
    o_ffnP = dout("o_ffnP", [2, 128, 32, 2])
    o_ffnS = dout("o_ffnS", [2, 128, 32, 4, 2])
    xscr = dscr("xscr", [D, TOK])
    hT_in = dscr("hT_in", [D, TOK], BF16)
    hT_all = dscr("hT_all", [NC * D, TOK], BF16)
    mix_in = dscr("mix_in", [NC * 384, TOK], BF16)
    mix_all = dscr("mix_all", [NC * NC * 384, TOK], BF16)
    halo_in = dscr("halo_in", [128, 64])
    halo_buf = dscr("halo_buf", [(NC + 1) * 128, 64])
    halo_all = dscr("halo_all", [NC * 128, 64])

    st = ExitStack()
    arena = st.enter_context(nc.sbuf_tensor("arena", [128, 48128], F32))
    cst = st.enter_context(nc.sbuf_tensor("cst", [128, 3300], F32))
    pss = [st.enter_context(nc.psum_tensor("ps%d" % i, [128, 512], F32)) for i in range(8)]
    PB = [Buf("ps%d" % i, excl=True) for i in range(8)]

    class Arena:
        def __init__(self, base, size):
            self.base, self.size, self.off = base, size, 0

        def reset(self):
            self.off = 0

        def alloc(self, shape, dt, name="t"):
            el = 2 if dt == BF16 else 4
            n = int(np.prod(shape[1:]))
            nb = (n * el + 31) // 32 * 32
            assert self.off + nb <= self.size, ("arena overflow", name, self.off, nb)
            w0 = (self.base + self.off) // 4
            ap = arena[:, w0:w0 + nb // 4]
            if dt == BF16:
                ap = ap.bitcast(BF16)
            ap = ap[:, 0:n]
            if len(shape) == 3:
                ap = ap.rearrange("p (a b) -> p a b", a=shape[1])
            elif len(shape) == 4:
                ap = ap.rearrange("p (a b c) -> p a b c", a=shape[1], b=shape[2])
            self.off += nb
            return ap, Buf(name)

    AR = Arena(0, 48128 * 4)

    coff = [0]

    def calloc(shape, dt):
        el = 2 if dt == BF16 else 4
        n = int(np.prod(shape[1:]))
        nb = (n * el + 31) // 32 * 32
        w0 = coff[0] // 4
        ap = cst[:, w0:w0 + nb // 4]
        if dt == BF16:
            ap = ap.bitcast(BF16)
        ap = ap[:, 0:n]
        if len(shape) == 3:
            ap = ap.rearrange("p (a b) -> p a b", a=shape[1])
        coff[0] += nb
        assert coff[0] <= 3300 * 4
        return ap

    CB = Buf("consts")
    cm32 = calloc([128, 8, 128], F32)
    cm16 = calloc([128, 5, 128], BF16)
    seq32 = calloc([128, 4], F32)
    am16 = calloc([128, 5, 512], BF16)
    fw.dma("sp", cm32, cmask[:, 0:8, :], writes=[CB])
    fw.dma("pool", cm16[:, 0:3, :], cmask[:, 0:3, :], writes=[CB])
    fw.dma("pool", cm16[:, 3:5, :], cmask[:, 8:10, :], writes=[CB])
    fw.dma("sp", seq32, seqm[:, :], writes=[CB])
    fw.dma("pool", am16, amask[:, :, :], writes=[CB])
    maskP16, maskS16, onesP16 = cm16[:, 0, :], cm16[:, 1, :], cm16[:, 2, :]
    tri16, ident16 = cm16[:, 3, :], cm16[:, 4, :]

    psrr = [0]

    def psum():
        i = psrr[0]
        psrr[0] = (i + 1) % 6
        return pss[i], PB[i]

    rank_sp = nc.sync.partition_id()

    class Rot:
        def __init__(self, shape, dt, k, name):
            self.items = [AR.alloc(shape, dt, name + str(i)) for i in range(k)]
            self.i = 0

        def get(self):
            r = self.items[self.i]
            self.i = (self.i + 1) % len(self.items)
            return r

    def rstd_from(srcs, src_bufs, n, nfeat, rots):
        sqrot, rsrot = rots
        ps, pb = pss[6], PB[6]
        k = len(srcs)
        for i, s in enumerate(srcs):
            sq, sqb = sqrot.get()
            fw.op("act", lambda h, sq=sq, s=s: h.activation(out=sq[:, 0:n], in_=s, func=AF.Square),
                  reads=src_bufs, writes=[sqb])
            fw.op("pe", lambda h, sq=sq, i=i: h.matmul(ps[:, 0:n], lhsT=onesP16, rhs=sq[:, 0:n],
                                                        start=(i == 0), stop=(i == k - 1)),
                  reads=[sqb, CB], writes=[pb])
        rs, rsb = rsrot.get()
        fw.op("act", lambda h: h.activation(out=rs[:, 0:n], in_=ps[:, 0:n], func=AF.Sqrt,
                                            scale=1.0 / nfeat, bias=EPSB[:, 0:1]),
              reads=[pb, CB], writes=[rsb])
        fw.op("dve", lambda h: h.reciprocal(out=rs[:, 0:n], in_=rs[:, 0:n]), reads=[rsb], writes=[rsb])
        return rs, rsb

    EPSB = calloc([128, 1], F32)
    fw.op("dve", lambda h: h.memset(EPSB, EPS), writes=[CB])
    AGFLAG = calloc([128, 1], F32)
    ZERO16 = calloc([128, 128], BF16)
    fw.op("dve", lambda h: h.memset(ZERO16, 0.0), writes=[CB])
    ONEB = calloc([128, 1], F32)
    fw.op("dve", lambda h: h.memset(ONEB, 1.0), writes=[CB])

    gn_t = calloc([128, 2, 40], F32)
    gs_t = calloc([128, 2, 16], F32)
    cw_t = calloc([128, 2, 16], F32)
    cb_t = calloc([128, 2, 4], F32)
    dtb_t = calloc([128, 2, 4], F32)
    al_t = calloc([128, 2, 4], F32)
    dk_t = calloc([128, 2, 2], F32)
    fcw_t = calloc([128, 2, 96], F32)
    fcb_t = calloc([128, 2, 32], F32)
    for l in range(2):
        fw.dma("sp", gn_t[:, l, :], gains[l].rearrange("p a b -> p (a b)"), writes=[CB])
        fw.dma("sp", gs_t[:, l, :], gssd[l].rearrange("p a b -> p (a b)"), writes=[CB])
        fw.dma("sp", cw_t[:, l, :], convw[l].rearrange("p a b -> p (a b)"), writes=[CB])
        fw.dma("sp", cb_t[:, l, :], convb[l], writes=[CB])
        fw.dma("sp", dtb_t[:, l, :], dtb[l], writes=[CB])
        fw.dma("sp", al_t[:, l, :], alog[l], writes=[CB])
        fw.dma("sp", dk_t[:, l, :], dskip[l], writes=[CB])
        fw.dma("sp", fcw_t[:, l, :], fcw[l].rearrange("p a b -> p (a b)"), writes=[CB])
        fw.dma("sp", fcb_t[:, l, :], fcb[l], writes=[CB])
    A_t = calloc([128, 2, 4], F32)
    fw.op("act", lambda h: h.activation(out=A_t, in_=al_t, func=AF.Exp), reads=[CB], writes=[CB])
    fw.op("dve", lambda h: h.tensor_scalar(out=A_t, in0=A_t, scalar1=-1.0, scalar2=None, op0=ALU.mult),
          reads=[CB], writes=[CB])

    def gain(l, which, kc):
        return gn_t[:, l, which * 8 + kc: which * 8 + kc + 1]

    TILES = [(0, 512), (512, 512), (1024, 512), (1536, 512), (2048, 128)]
    DRAMB = {n: Buf(n) for n in ["xscr", "hT_in", "hT_all", "mix_in", "mix_all", "halo_in", "halo_buf", "halo_all", "outs"]}

    def d1_tile(l, xt, xtb, c0, n, rots, htile):
        ht, htb = htile
        rs, rsb = rstd_from([xt[:, kc, 0:n] for kc in range(8)], [xtb], n, D, rots)
        for kc in range(8):
            fw.op("dve", lambda h, kc=kc: h.scalar_tensor_tensor(
                out=ht[:, kc, 0:n], in0=xt[:, kc, 0:n], scalar=gain(l, 0, kc), in1=rs[:, 0:n],
                op0=ALU.mult, op1=ALU.mult), reads=[xtb, rsb, CB], writes=[htb])
        fw.dma("sp", hT_in.rearrange("(k p) t -> p k t", p=128)[:, :, c0:c0 + n], ht[:, :, 0:n],
               reads=[htb], writes=[DRAMB["hT_in"]])

    def allgather(src, dst, sb, db):
        if os.environ.get("MK_NOAG"):
            return
        fw.custom("pool", lambda h: h.collective_compute(
            "AllGather", ALU.bypass, replica_groups=[list(range(NC))], ins=[src.opt()], outs=[dst.opt()]),
            reads=[sb], writes=[db])
        fw.op("pool", lambda h: h.memset(AGFLAG, 0.0), reads=[db], writes=[db])

    def phase_d1_l0():
        AR.reset()
        rots = (Rot([128, 512], BF16, 2, "sq"), Rot([128, 512], F32, 2, "rs"))
        xr = Rot([128, 8, 512], F32, 2, "xt")
        hr = Rot([128, 8, 512], BF16, 2, "ht")
        for (c0, n) in TILES:
            xt, xtb = xr.get()
            fw.dma("sp", xt[:, :, 0:n], xT.rearrange("(k p) t -> p k t", p=128)[:, :, c0:c0 + n], writes=[xtb])
            d1_tile(0, xt, xtb, c0, n, rots, hr.get())

    def phase_m(l):
        print("MARK phase_m start nrec", getattr(fw, "nrec", 0), flush=True)
        fw.barrier()
        AR.reset()
        qT, qTb = AR.alloc([128, NT], BF16, "qT")
        kT, kTb = AR.alloc([128, NT], BF16, "kT")
        vres, vrb = AR.alloc([128, 136, 128], BF16, "vres")
        mark_m = AR.off
        wfm, wfmb = AR.alloc([128, 8, 1024], BF16, "wfm")
        wtm, wtmb = AR.alloc([128, 8, 132], BF16, "wtm")
        fw.dma("pool", wfm, w_fm[l].rearrange("(k p) m -> p k m", p=128), writes=[wfmb])
        fw.dma("pool", wtm, w_tm[l].rearrange("(k p) m -> p k m", p=128), writes=[wtmb])
        hrot = Rot([128, 8, 512], BF16, 2, "hblk")
        xpadP, xpPb = AR.alloc([128, 4, 16 * 35], F32, "xpad")
        xpadS, xpSb = xpadP, xpPb
        xc, xcb = AR.alloc([128, 4, 512], BF16, "xc")
        cacc, caccb = AR.alloc([128, 512], F32, "cacc")
        zs, zsb = AR.alloc([128, 2, 512], BF16, "zs")
        k32r = Rot([128, 512], F32, 2, "k32")
        v32r = Rot([128, 128], F32, 2, "v32")
        gbr = Rot([128, 2, 128], BF16, 2, "gb")
        hT32, hT32b = AR.alloc([128, 256], F32, "hT32")
        hTz, hTzb = AR.alloc([128, 4, 128], BF16, "hTz")
        h0z, h0zb = AR.alloc([128, 4, 4, 128], BF16, "h0z")
        h0pt, h0ptb = AR.alloc([128, 4, 2, 128], F32, "h0pt")
        xdt, xdtb = AR.alloc([128, 4, 128], BF16, "xdt")
        xde, xdeb = AR.alloc([128, 256], BF16, "xde")
        xdem, xdemb = AR.alloc([128, 256], BF16, "xdem")
        xstm, xstmb = AR.alloc([128, 256], BF16, "xstm")
        Btm, Btmb = AR.alloc([128, 128], BF16, "Btm")
        sm, smb = AR.alloc([128, 64], F32, "small")
        dt_ = sm[:, 0:4]; a_ = sm[:, 4:8]; acum = sm[:, 8:12]; atot = sm[:, 12:16]
        dte = sm[:, 16:20]; dtd = sm[:, 20:24]; dct = sm[:, 24:28]; tmp4 = sm[:, 28:32]
        dsel = sm[:, 32:48]
        CBm, CBmb = AR.alloc([128, 128], F32, "CBm")
        tria, triab = AR.alloc([128, 4, 128], F32, "tria")
        seg, segb = AR.alloc([128, 4, 128], F32, "seg")
        Ebc, Ebcb = AR.alloc([128, 4, 128], F32, "Ebc")
        MT, MTb = AR.alloc([128, 4, 128], BF16, "MT")
        Cs, Csb = AR.alloc([128, 4, 128], BF16, "Cs")
        yg, ygb = AR.alloc([128, 2, 128], F32, "yg")
        hend, hendb = AR.alloc([128, 4, 2, 128], F32, "hend")
        tail3, tail3b = AR.alloc([128, 4, 16, 3], F32, "tail3")
        fw.op("pool", lambda h: h.memset(hT32, 0.0), writes=[hT32b])
        fw.op("pool", lambda h: h.memset(hTz, 0.0), writes=[hTzb])
        fw.op("pool", lambda h: h.memset(h0z, 0.0), writes=[h0zb])
        fw.op("pool", lambda h: h.memset(xdt, 0.0), writes=[xdtb])
        fw.op("pool", lambda h: h.memset(xpadP[:, :, 0:3], 0.0), writes=[xpPb])
        print("MARK m-setup-done nrec", fw.nrec, flush=True)
        hall = hT_all.rearrange("(r k p) t -> r p k t", r=NC, p=128)
        mixv = mix_in.rearrange("(o q) t -> o q t", o=NC)

        for sbi in range(34):
            if SBSEL and str(sbi) not in SBSEL.split(","):
                continue
            samp = sbi >= 32
            hb, hbb = hrot.get()
            if not samp:
                r, cc = sbi // 4, (sbi % 4) * 512
                fw.dma("sp", hb, hall[r][:, :, cc:cc + 512], reads=[DRAMB["hT_all"]], writes=[hbb])
                g0 = sbi * 512
            else:
                for i in range(4):
                    r = (sbi - 32) * 4 + i
                    fw.dma("sp", hb[:, :, i * 128:(i + 1) * 128], hall[r][:, :, 2048:2176],
                           reads=[DRAMB["hT_all"]], writes=[hbb])
                g0 = 16384 + (sbi - 32) * 512
            xpad = xpadS if samp else xpadP
            xpb = xpSb if samp else xpPb
            nseq, L = (16, 32) if samp else (1, 512)
            xpv = [xpad[:, c, 0:nseq * (L + 3)].rearrange("p (s t) -> p s t", s=nseq) for c in range(4)]
            if samp:
                s0 = (sbi - 32) * 16
                for c in range(4):
                    fw.dma("sp", xpv[c][:, :, 0:3], conv0[l][:, c, s0:s0 + 16, :], writes=[xpb])
            for c in range(8):
                ps, pb = psum()
                for kc in range(8):
                    fw.op("pe", lambda h, ps=ps, c=c, kc=kc, hb=hb: h.matmul(
                        ps[:, :], lhsT=wfm[:, kc, c * 128:(c + 1) * 128], rhs=hb[:, kc, :],
                        start=(kc == 0), stop=(kc == 7)), reads=[wfmb, hbb], writes=[pb])
                if c == 0:
                    fw.op("act", lambda h, ps=ps: h.activation(out=qT[:, g0:g0 + 512], in_=ps[:, :], func=AF.Copy),
                          reads=[pb], writes=[qTb])
                elif c == 1:
                    k32, k32b = k32r.get()
                    fw.op("act", lambda h, ps=ps: h.activation(out=kT[:, g0:g0 + 512], in_=ps[:, :], func=AF.Copy),
                          reads=[pb], writes=[kTb])
                    fw.op("dve", lambda h, ps=ps, k32=k32: h.tensor_copy(out=k32, in_=ps[:, :]), reads=[pb], writes=[k32b])
                    fw.dma("sp", o_kT[l][:, g0:g0 + 512], k32, reads=[k32b], writes=[DRAMB["outs"]])
                elif c in (2, 3):
                    fw.op("act", lambda h, ps=ps, c=c: h.activation(out=zs[:, c - 2, :], in_=ps[:, :], func=AF.Silu),
                          reads=[pb], writes=[zsb])
                else:
                    fw.op("dve", lambda h, ps=ps, c=c: h.tensor_copy(
                        out=xpv[c - 4][:, :, 3:3 + L], in_=ps[:, :].rearrange("p (s t) -> p s t", s=nseq)),
                        reads=[pb], writes=[xpb])
            print("MARK proj-done sbi", sbi, fw.nrec, flush=True)
            for c in range(4):
                cav = cacc[:, :].rearrange("p (s t) -> p s t", s=nseq)
                fw.op("dve", lambda h, c=c, cav=cav: h.tensor_scalar(
                    out=cav, in0=xpv[c][:, :, 0:L], scalar1=cw_t[:, l, c * 4:c * 4 + 1],
                    scalar2=cb_t[:, l, c:c + 1], op0=ALU.mult, op1=ALU.add), reads=[xpb, CB], writes=[caccb])
                for tap in (1, 2, 3):
                    fw.op("dve", lambda h, c=c, tap=tap, cav=cav: h.scalar_tensor_tensor(
                        out=cav, in0=xpv[c][:, :, tap:tap + L], scalar=cw_t[:, l, c * 4 + tap:c * 4 + tap + 1],
                        in1=cav, op0=ALU.mult, op1=ALU.add), reads=[xpb, caccb, CB], writes=[caccb])
                fw.op("act", lambda h, c=c: h.activation(out=xc[:, c, :], in_=cacc[:, :], func=AF.Silu),
                      reads=[caccb], writes=[xcb])
            print("MARK conv-done sbi", sbi, fw.nrec, flush=True)
            if samp:
                for c in range(4):
                    fw.op("pool", lambda h, c=c: h.tensor_copy(out=tail3[:, c, :, :], in_=xpv[c][:, :, L:L + 3]),
                          reads=[xpb], writes=[tail3b])
                fw.dma("sp", o_convS[l][:, :, s0:s0 + 16, :], tail3, reads=[tail3b], writes=[DRAMB["outs"]])
            else:
                if sbi == 31:
                    for c in range(4):
                        fw.op("pool", lambda h, c=c: h.tensor_copy(out=tail3[:, c, 0, :], in_=xpv[c][:, 0, L:L + 3]),
                              reads=[xpb], writes=[tail3b])
                    fw.dma("sp", o_convP[l], tail3[:, :, 0, :], reads=[tail3b], writes=[DRAMB["outs"]])
                for c in range(4):
                    fw.op("pool", lambda h, c=c: h.tensor_copy(out=xpv[c][:, 0, 0:3], in_=xpv[c][:, 0, L:L + 3]),
                          reads=[xpb], writes=[xpb])
            for bi in range(4 if LVL >= 2 else 0):
                t0 = bi * 128
                gcol = g0 + t0
                blk = gcol // 128
                ps, pb = psum()
                for kc in range(8):
                    fw.op("pe", lambda h, ps=ps, kc=kc, hb=hb, t0=t0: h.matmul(
                        ps[:, 0:132], lhsT=hb[:, kc, t0:t0 + 128], rhs=wtm[:, kc, :],
                        start=(kc == 0), stop=(kc == 7)), reads=[wtmb, hbb], writes=[pb])
                v32, v32b = v32r.get()
                fw.op("act", lambda h, ps=ps, blk=blk: h.activation(out=vres[:, blk, :], in_=ps[:, 0:128], func=AF.Copy),
                      reads=[pb], writes=[vrb])
                fw.op("dve", lambda h, ps=ps, v32=v32: h.tensor_copy(out=v32, in_=ps[:, 0:128]), reads=[pb], writes=[v32b])
                fw.dma("sp", o_v[l][gcol:gcol + 128, :], v32, reads=[v32b], writes=[DRAMB["outs"]])
                fw.op("dve", lambda h, ps=ps: h.tensor_tensor(out=tmp4, in0=ps[:, 128:132], in1=dtb_t[:, l, :], op=ALU.add),
                      reads=[pb, CB], writes=[smb])
                fw.op("act", lambda h: h.activation(out=tmp4, in_=tmp4, func=AF.Exp), reads=[smb], writes=[smb])
                fw.op("act", lambda h: h.activation(out=dt_, in_=tmp4, func=AF.Ln, bias=ONEB[:, 0:1], scale=1.0),
                      reads=[smb, CB], writes=[smb])
                fw.op("dve", lambda h: h.tensor_tensor(out=a_, in0=dt_, in1=A_t[:, l, :], op=ALU.mult),
                      reads=[smb, CB], writes=[smb])
                mk32 = cm32[:, 1 if samp else 0, :]
                on32 = cm32[:, 3 if samp else 2, :]
                mk16 = maskS16 if samp else maskP16
                pst, pstb = psum()
                pstv = pst[:, 0:192].bitcast(BF16)
                for j in range(3):
                    fw.op("pe", lambda h, j=j, pstv=pstv, t0=t0: h.transpose(
                        pstv[:, j * 128:(j + 1) * 128], xc[:, j, t0:t0 + 128], ident16), reads=[xcb, CB], writes=[pstb])
                fw.op("act", lambda h, pstv=pstv: h.activation(out=xstm, in_=pstv[:, 0:256], func=AF.Copy),
                      reads=[pstb], writes=[xstmb])
                fw.op("dve", lambda h, pstv=pstv: h.tensor_copy(out=Btm, in_=pstv[:, 256:384]), reads=[pstb], writes=[Btmb])
                ps2, pb2 = psum()
                fw.op("pe", lambda h, ps2=ps2, t0=t0: h.matmul(ps2[:, 0:128], lhsT=xc[:, 2, t0:t0 + 128],
                                                               rhs=xc[:, 3, t0:t0 + 128], start=True, stop=True),
                      reads=[xcb], writes=[pb2])
                fw.op("dve", lambda h, ps2=ps2, mk32=mk32: h.tensor_tensor(out=CBm, in0=ps2[:, 0:128], in1=mk32, op=ALU.mult),
                      reads=[pb2, CB], writes=[CBmb])
                ps3, pb3 = psum()
                fw.op("pe", lambda h, ps3=ps3, mk32=mk32: h.matmul(ps3[:, 0:4], lhsT=mk32, rhs=a_, start=True, stop=True),
                      reads=[smb, CB], writes=[pb3])
                fw.op("pe", lambda h, ps3=ps3, on32=on32: h.matmul(ps3[:, 4:8], lhsT=on32, rhs=a_, start=True, stop=True),
                      reads=[smb, CB], writes=[pb3])
                fw.op("dve", lambda h, ps3=ps3: h.tensor_copy(out=sm[:, 8:16], in_=ps3[:, 0:8]), reads=[pb3], writes=[smb])
                for hh in range(4):
                    fw.op("pool", lambda h, hh=hh, mk32=mk32: h.tensor_scalar(
                        out=tria[:, hh, :], in0=mk32, scalar1=a_[:, hh:hh + 1], scalar2=None, op0=ALU.mult),
                        reads=[smb, CB], writes=[triab])
                ps4, pb4 = psum()
                fw.op("pe", lambda h, ps4=ps4: h.matmul(ps4[:, :], lhsT=cm32[:, 2, :],
                                                        rhs=tria[:, :, :].rearrange("p a b -> p (a b)"),
                                                        start=True, stop=True), reads=[triab, CB], writes=[pb4])
                for hh in range(4):
                    fw.op("dve", lambda h, hh=hh, ps4=ps4: h.tensor_scalar(
                        out=seg[:, hh, :], in0=ps4[:, hh * 128:(hh + 1) * 128], scalar1=acum[:, hh:hh + 1],
                        scalar2=0.0, op0=ALU.subtract, op1=ALU.min), reads=[pb4, smb], writes=[segb])
                fw.op("act", lambda h: h.activation(out=seg, in_=seg, func=AF.Exp), reads=[segb], writes=[segb])
                fw.op("act", lambda h, ps4=ps4: h.activation(out=Ebc[:, :, :].rearrange("p a b -> p (a b)"),
                                                             in_=ps4[:, :], func=AF.Exp), reads=[pb4], writes=[Ebcb])
                fw.op("dve", lambda h: h.tensor_tensor(out=tmp4, in0=atot, in1=acum, op=ALU.subtract), reads=[smb], writes=[smb])
                fw.op("act", lambda h: h.activation(out=dte, in_=tmp4, func=AF.Exp), reads=[smb], writes=[smb])
                fw.op("act", lambda h: h.activation(out=dct, in_=atot, func=AF.Exp), reads=[smb], writes=[smb])
                fw.op("dve", lambda h: h.tensor_tensor(out=dtd, in0=dt_, in1=dte, op=ALU.mult), reads=[smb], writes=[smb])
                for hh in range(4):
                    fw.op("dve", lambda h, hh=hh: h.tensor_tensor(out=MT[:, hh, :], in0=seg[:, hh, :], in1=CBm, op=ALU.mult),
                          reads=[segb, CBmb], writes=[MTb])
                    fw.op("pool", lambda h, hh=hh, t0=t0: h.tensor_tensor(out=Cs[:, hh, :], in0=Ebc[:, hh, :],
                                                                          in1=xc[:, 3, t0:t0 + 128], op=ALU.mult),
                          reads=[Ebcb, xcb], writes=[Csb])
                    c0_ = (hh % 2) * 64
                    fw.op("dve", lambda h, hh=hh, c0_=c0_: h.tensor_scalar(
                        out=xdt[:, hh, c0_:c0_ + 64], in0=xstm[:, hh * 64:(hh + 1) * 64], scalar1=dt_[:, hh:hh + 1],
                        scalar2=None, op0=ALU.mult), reads=[xstmb, smb], writes=[xdtb])
                    fw.op("pool", lambda h, hh=hh: h.tensor_scalar(
                        out=xde[:, hh * 64:(hh + 1) * 64], in0=xstm[:, hh * 64:(hh + 1) * 64], scalar1=dtd[:, hh:hh + 1],
                        scalar2=None, op0=ALU.mult), reads=[xstmb, smb], writes=[xdeb])
                if samp:
                    sq0 = (blk - 128) * 4
                    for j in range(4):
                        for hh in range(4):
                            c0_ = (hh % 2) * 64
                            fw.dma("pool", h0z[:, j, hh, c0_:c0_ + 64], h0T[l][sq0 + j][:, hh * 64:(hh + 1) * 64],
                                   writes=[h0zb])
                        fw.dma("sp", h0pt[:, j, :, :], h0p[l][sq0 + j].rearrange("a p n -> p a n"), writes=[h0ptb])
                gb, gbb = gbr.get()
                for pr in range(2):
                    psy, pby = psum()
                    first = True
                    for hh in (2 * pr, 2 * pr + 1):
                        fw.op("pe", lambda h, psy=psy, hh=hh, first=first: h.matmul(
                            psy[:, 0:128], lhsT=xdt[:, hh, :], rhs=MT[:, hh, :], start=first, stop=False),
                            reads=[xdtb, MTb], writes=[pby])
                        first = False
                        if not samp:
                            fw.op("pe", lambda h, psy=psy, hh=hh: h.matmul(
                                psy[:, 0:128], lhsT=hTz[:, hh, :], rhs=Cs[:, hh, :], start=False, stop=(hh % 2 == 1)),
                                reads=[hTzb, Csb], writes=[pby])
                        else:
                            for j in range(4):
                                fw.op("pe", lambda h, psy=psy, hh=hh, j=j: h.matmul(
                                    psy[:, j * 32:(j + 1) * 32], lhsT=h0z[:, j, hh, :], rhs=Cs[:, hh, j * 32:(j + 1) * 32],
                                    start=False, stop=(hh % 2 == 1 and j == 3)), reads=[h0zb, Csb], writes=[pby])
                    fw.op("dve", lambda h, psy=psy, pr=pr, t0=t0: h.scalar_tensor_tensor(
                        out=yg[:, pr, :], in0=xc[:, pr, t0:t0 + 128], scalar=dk_t[:, l, pr:pr + 1], in1=psy[:, 0:128],
                        op0=ALU.mult, op1=ALU.add), reads=[xcb, pby, CB], writes=[ygb])
                    fw.op("dve", lambda h, pr=pr, t0=t0, gb=gb: h.tensor_tensor(
                        out=gb[:, pr, :], in0=yg[:, pr, :], in1=zs[:, pr, t0:t0 + 128], op=ALU.mult),
                        reads=[ygb, zsb], writes=[gbb])
                if not samp:
                    own, lc = blk // 16, (blk % 16) * 128
                else:
                    own, lc = blk - 128, 2048
                fw.dma("sp", mixv[own][0:256, lc:lc + 128].rearrange("(j p) t -> p j t", p=128), gb,
                       reads=[gbb], writes=[DRAMB["mix_in"]])
                if LVL < 3:
                    continue
                if not samp:
                    pss_, pbs = psum()
                    fw.op("pe", lambda h, pss_=pss_: h.matmul(pss_[:, 0:256], lhsT=Btm, rhs=xde, start=True, stop=True),
                          reads=[Btmb, xdeb], writes=[pbs])
                    for hh in range(4):
                        fw.op("dve", lambda h, hh=hh, pss_=pss_: h.scalar_tensor_tensor(
                            out=hT32[:, hh * 64:(hh + 1) * 64], in0=hT32[:, hh * 64:(hh + 1) * 64], scalar=dct[:, hh:hh + 1],
                            in1=pss_[:, hh * 64:(hh + 1) * 64], op0=ALU.mult, op1=ALU.add),
                            reads=[pbs, smb, hT32b], writes=[hT32b])
                        c0_ = (hh % 2) * 64
                        fw.op("pool", lambda h, hh=hh, c0_=c0_: h.tensor_copy(out=hTz[:, hh, c0_:c0_ + 64],
                                                                               in_=hT32[:, hh * 64:(hh + 1) * 64]),
                              reads=[hT32b], writes=[hTzb])
                    if blk == 127:
                        fw.dma("sp", o_ssdP[l], hT32, reads=[hT32b], writes=[DRAMB["outs"]])
                else:
                    for j in range(4):
                        psd, pbd = psum()
                        fw.op("pe", lambda h, psd=psd, j=j: h.matmul(psd[:, 0:4], lhsT=SEL32[:, j, :],
                                                                      rhs=atot, start=True, stop=True),
                              reads=[smb, CB], writes=[pbd])
                        fw.op("act", lambda h, psd=psd, j=j: h.activation(out=dsel[:, j * 4:(j + 1) * 4], in_=psd[:, 0:4], func=AF.Exp),
                              reads=[pbd], writes=[smb])
                        fw.op("dve", lambda h, j=j: h.tensor_scalar(out=xdem, in0=xde, scalar1=seq32[:, j:j + 1], scalar2=None,
                                                                    op0=ALU.mult), reads=[xdeb, CB], writes=[xdemb])
                        for pr in range(2):
                            pse, pbe = psum()
                            fw.op("pe", lambda h, pse=pse, pr=pr: h.matmul(pse[:, 0:128], lhsT=xdem[:, pr * 128:(pr + 1) * 128],
                                                                            rhs=Btm, start=True, stop=True),
                                  reads=[xdemb, Btmb], writes=[pbe])
                            for hf in range(2):
                                hh = 2 * pr + hf
                                rs_ = slice(hf * 64, hf * 64 + 64)
                                fw.op("dve", lambda h, pse=pse, pr=pr, j=j, hh=hh, rs_=rs_: h.scalar_tensor_tensor(
                                    out=hend[rs_, j, pr, :], in0=h0pt[rs_, j, pr, :], scalar=dsel[rs_, j * 4 + hh:j * 4 + hh + 1],
                                    in1=pse[rs_, 0:128], op0=ALU.mult, op1=ALU.add),
                                    reads=[pbe, smb, h0ptb], writes=[hendb])
                    fw.dma("sp", o_ssdS[l][sq0:sq0 + 4].rearrange("s a p n -> p s a n"), hend,
                           reads=[hendb], writes=[DRAMB["outs"]])
        if STAGE < 3:
            return
        fw.barrier()
        AR.off = mark_m
        ztr = Rot([128, 512], F32, 2, "e32")
        spr = Rot([128, 512], BF16, 2, "sp16")
        LL32, LL32b = AR.alloc([128, 512], F32, "LL32")
        LL16r = Rot([128, 512], BF16, 2, "LL16")
        tr = Rot([128, 512], F32, 2, "t32")
        Wr = Rot([128, 512], BF16, 2, "W16")
        kcr = Rot([128, 16, 128], BF16, 2, "kc")
        vcr = Rot([128, 16, 128], BF16, 2, "vcc")
        o16r = Rot([128, 512], BF16, 2, "o16")
        SC = 128.0 ** -0.5
        for qg in range(34):
            if SBSEL and str(qg) not in SBSEL.split(","):
                continue
            samp = qg >= 32
            q0 = qg * 512
            if not samp:
                steps = [("p", kb) for kb in range(4 * qg + 3, -1, -1)]
            else:
                steps = [("n", 0)] + [("c", kb) for kb in range(15, -1, -1)]
            po, pob = pss[7], PB[7]
            for si, (kind, kb) in enumerate(steps):
                pz, pzb = psum()
                ll16, ll16b = LL16r.get()
                if kind == "p":
                    fw.op("pe", lambda h, pz=pz, kb=kb: h.matmul(pz[:, :], lhsT=kT[:, kb * 128:(kb + 1) * 128],
                                                                 rhs=qT[:, q0:q0 + 512], start=True, stop=True),
                          reads=[kTb, qTb], writes=[pzb])
                    masked = kb >= 4 * qg
                    mk = am16[:, kb - 4 * qg, :] if masked else None
                elif kind == "n":
                    for b4 in range(4):
                        cs = q0 + b4 * 128
                        fw.op("pe", lambda h, pz=pz, cs=cs, b4=b4: h.matmul(
                            pz[:, b4 * 128:(b4 + 1) * 128], lhsT=kT[:, cs:cs + 128], rhs=qT[:, cs:cs + 128],
                            start=True, stop=True), reads=[kTb, qTb], writes=[pzb])
                    masked, mk = True, am16[:, 4, :]
                else:
                    kcb, kcbb = kcr.get()
                    vcb, vcbb = vcr.get()
                    fw.dma("pool", kcb, kcT[l][qg - 32][kb], writes=[kcbb])
                    fw.dma("pool", vcb, vc[l][qg - 32][kb], writes=[vcbb])
                    for s in range(16):
                        fw.op("pe", lambda h, pz=pz, s=s, kcb=kcb: h.matmul(
                            pz[:, s * 32:(s + 1) * 32], lhsT=kcb[:, s, :], rhs=qT[:, q0 + s * 32:q0 + (s + 1) * 32],
                            start=True, stop=True), reads=[kcbb, qTb], writes=[pzb])
                    masked, mk = False, None
                e32, e32b = ztr.get()
                sp16, sp16b = spr.get()
                fw.op("act", lambda h, pz=pz, e32=e32: h.activation(out=e32, in_=pz[:, :], func=AF.Exp, scale=SC),
                      reads=[pzb], writes=[e32b])
                fw.op("act", lambda h, e32=e32, sp16=sp16: h.activation(out=sp16, in_=e32, func=AF.Ln, bias=ONEB[:, 0:1], scale=1.0),
                      reads=[e32b, CB], writes=[sp16b])
                if masked:
                    fw.op("pool", lambda h, sp16=sp16, mk=mk: h.tensor_tensor(out=sp16, in0=sp16, in1=mk, op=ALU.mult),
                          reads=[sp16b, CB], writes=[sp16b])
                pS, pSb = psum()
                fw.op("pe", lambda h, pS=pS, sp16=sp16: h.matmul(pS[:, :], lhsT=tri16, rhs=sp16, start=True, stop=(si == 0)),
                      reads=[sp16b, CB], writes=[pSb])
                if si > 0:
                    fw.op("pe", lambda h, pS=pS, ll16p=ll16p: h.matmul(pS[:, :], lhsT=onesP16, rhs=ll16p, start=False, stop=True),
                          reads=[ll16pb, CB], writes=[pSb])
                t32, t32b = tr.get()
                fw.op("dve", lambda h, pz=pz, sp16=sp16, t32=t32: h.scalar_tensor_tensor(
                    out=t32, in0=pz[:, :], scalar=-SC, in1=sp16, op0=ALU.mult, op1=ALU.add),
                    reads=[pzb, sp16b], writes=[t32b])
                fw.op("dve", lambda h, pS=pS, t32=t32: h.tensor_tensor(out=t32, in0=t32, in1=pS[:, :], op=ALU.add),
                      reads=[pSb, t32b], writes=[t32b])
                W16, W16b = Wr.get()
                fw.op("act", lambda h, t32=t32, W16=W16: h.activation(out=W16, in_=t32, func=AF.Exp, scale=-1.0),
                      reads=[t32b], writes=[W16b])
                if masked:
                    fw.op("pool", lambda h, W16=W16, mk=mk: h.tensor_tensor(out=W16, in0=W16, in1=mk, op=ALU.mult),
                          reads=[W16b, CB], writes=[W16b])
                if si < len(steps) - 1:
                    if si == 0:
                        fw.op("pool", lambda h, sp16=sp16: h.tensor_copy(out=LL32, in_=sp16), reads=[sp16b], writes=[LL32b])
                    else:
                        fw.op("pool", lambda h, sp16=sp16: h.tensor_tensor(out=LL32, in0=LL32, in1=sp16, op=ALU.add),
                              reads=[sp16b, LL32b], writes=[LL32b])
                    fw.op("pool", lambda h, ll16=ll16: h.tensor_copy(out=ll16, in_=LL32), reads=[LL32b], writes=[ll16b])
                    ll16p, ll16pb = ll16, ll16b
                last = si == len(steps) - 1
                if kind == "p":
                    fw.op("pe", lambda h, kb=kb, W16=W16, si=si, last=last: h.matmul(
                        po[:, :], lhsT=vres[:, kb, :], rhs=W16, start=(si == 0), stop=last),
                        reads=[vrb, W16b], writes=[pob])
                elif kind == "n":
                    fw.op("pe", lambda h, W16=W16: h.matmul(po[:, :], lhsT=ZERO16, rhs=W16, start=True, stop=False),
                          reads=[CB, W16b], writes=[pob])
                    for b4 in range(4):
                        blk = (q0 + b4 * 128) // 128
                        fw.op("pe", lambda h, blk=blk, W16=W16, b4=b4: h.matmul(
                            po[:, b4 * 128:(b4 + 1) * 128], lhsT=vres[:, blk, :], rhs=W16[:, b4 * 128:(b4 + 1) * 128],
                            start=False, stop=False), reads=[vrb, W16b], writes=[pob])
                else:
                    for s in range(16):
                        fw.op("pe", lambda h, s=s, W16=W16, vcb=vcb, last=last: h.matmul(
                            po[:, s * 32:(s + 1) * 32], lhsT=vcb[:, s, :], rhs=W16[:, s * 32:(s + 1) * 32],
                            start=False, stop=(last and s == 15)), reads=[vcbb, W16b], writes=[pob])
            o16, o16b = o16r.get()
            fw.op("act", lambda h, o16=o16: h.activation(out=o16, in_=po[:, :], func=AF.Copy), reads=[pob], writes=[o16b])
            for b4 in range(4):
                blk = (q0 + b4 * 128) // 128
                if not samp:
                    own, lc = blk // 16, (blk % 16) * 128
                else:
                    own, lc = blk - 128, 2048
                fw.dma("sp", mixv[own][256:384, lc:lc + 128], o16[:, b4 * 128:(b4 + 1) * 128],
                       reads=[o16b], writes=[DRAMB["mix_in"]])

    SEL32 = cm32[:, 4:8, :]

    def phase_d2(l):
        fw.barrier()
        AR.reset()
        rots = (Rot([128, 512], BF16, 2, "sq"), Rot([128, 512], F32, 2, "rs"))
        h2T, h2Tb = AR.alloc([128, 8, TOK], BF16, "h2T")
        gtl, gtlb = AR.alloc([128, 32, 5, 2], F32, "gtail")
        wrot = Rot([128, 32, 128], BF16, 3, "w")
        wrot2 = Rot([128, 8, 128], BF16, 3, "w2")
        stg, stgb = AR.alloc([128, 8, 512], F32, "stg")
        xt, xtb = AR.alloc([128, 8, 512], F32, "xt")
        tl16, tl16b = AR.alloc([128, 8, 8], BF16, "tl16")
        mark = AR.off
        mall = mix_all.rearrange("(r o q) t -> o q r t", r=NC, o=NC)
        xsrc = xT if l == 0 else xscr
        xdst = xscr if l == 0 else yT
        xsv = xsrc.rearrange("(k p) t -> p k t", p=128)
        xscv = xscr.rearrange("(k p) t -> p k t", p=128)
        xdv = xdst.rearrange("(k p) t -> p k t", p=128)

        def wload(rot, wap, KC, view=None):
            wt, wtb = rot.get()
            src = wap.rearrange("(k p) m -> p k m", p=128) if view is None else view
            fw.dma("pool", wt[:, 0:KC, :], src, writes=[wtb])
            return wt, wtb

        def mm(ps, pb, wt, wtb, KC, rhs, rbufs, n, start=True, stop=True):
            for kc in range(KC):
                fw.op("pe", lambda h, kc=kc: h.matmul(ps[:, 0:n], lhsT=wt[:, kc, :], rhs=rhs(kc),
                                                      start=(start and kc == 0), stop=(stop and kc == KC - 1)),
                      reads=[wtb] + rbufs, writes=[pb])

        def evac_norm_add(ps, pb, m, n, sqrot_unused=None):
            fw.op("dve", lambda h: h.tensor_copy(out=stg[:, m, 0:n], in_=ps[:, 0:n]), reads=[pb], writes=[stgb])

        def post_norm_add(l, which, n):
            rs, rsb = rstd_from([stg[:, m, 0:n] for m in range(8)], [stgb], n, D, rots)
            for m in range(8):
                fw.op("dve", lambda h, m=m: h.scalar_tensor_tensor(
                    out=stg[:, m, 0:n], in0=stg[:, m, 0:n], scalar=gain(l, which, m), in1=rs[:, 0:n],
                    op0=ALU.mult, op1=ALU.mult), reads=[stgb, rsb, CB], writes=[stgb])
                fw.op("dve", lambda h, m=m: h.tensor_tensor(out=xt[:, m, 0:n], in0=xt[:, m, 0:n], in1=stg[:, m, 0:n], op=ALU.add),
                      reads=[stgb, xtb], writes=[xtb])

        AR.off = mark
        ht, htb = AR.alloc([128, 8, 512], BF16, "ht")
        gT, gTb = AR.alloc([128, 2, 8, 512], BF16, "gT")
        oT, oTb = AR.alloc([128, 8, 512], BF16, "oT")
        mer, merb = AR.alloc([128, 8, 512], BF16, "mer")
        sig, sigb = AR.alloc([128, 512], F32, "sig")
        mtmp, mtmpb = AR.alloc([128, 512], F32, "mtmp")
        for ti, (c0, n) in enumerate(TILES):
            fw.dma("sp", xt[:, :, 0:n], xsv[:, :, c0:c0 + n], reads=[DRAMB["xscr"]], writes=[xtb])
            fw.dma("sp", ht[:, :, 0:n], hT_in.rearrange("(k p) t -> p k t", p=128)[:, :, c0:c0 + n],
                   reads=[DRAMB["hT_in"]], writes=[htb])
            for j in range(2):
                fw.dma("sp", gT[:, j, :, 0:n],
                       mall[bass.ds(rank_sp, 1), j * 128:(j + 1) * 128, :, c0:c0 + n].rearrange("o q r t -> (o q) r t"),
                       reads=[DRAMB["mix_all"]], writes=[gTb])
            fw.dma("sp", oT[:, :, 0:n],
                   mall[bass.ds(rank_sp, 1), 256:384, :, c0:c0 + n].rearrange("o q r t -> (o q) r t"),
                   reads=[DRAMB["mix_all"]], writes=[oTb])
            for G in range(4):
                srcs = [gT[:, j, r, 0:n] for r in (2 * G, 2 * G + 1) for j in range(2)]
                rs, rsb = rstd_from(srcs, [gTb], n, 512, rots)
                for r in (2 * G, 2 * G + 1):
                    for j in range(2):
                        fw.op("dve", lambda h, j=j, r=r, rs=rs: h.scalar_tensor_tensor(
                            out=gT[:, j, r, 0:n], in0=gT[:, j, r, 0:n], scalar=gs_t[:, l, j * 8 + r:j * 8 + r + 1],
                            in1=rs[:, 0:n], op0=ALU.mult, op1=ALU.mult), reads=[gTb, rsb, CB], writes=[gTb])
            for m in range(8):
                ms = slice(m * 128, (m + 1) * 128)
                wA, wAb = wrot.get()
                for j_ in range(2):
                    fw.dma("pool", wA[:, j_ * 8:(j_ + 1) * 8, :],
                           w_brssd[l].rearrange("(r j p) m -> p j r m", r=8, j=2)[:, j_, :, ms], writes=[wAb])
                wG, wGb = wload(wrot2, w_gate[l][:, ms], 8)
                psA, pbA = psum()
                mm(psA, pbA, wA, wAb, 16, lambda kc: gT[:, kc // 8, kc % 8, 0:n], [gTb], n)
                psG, pbG = psum()
                mm(psG, pbG, wG, wGb, 8, lambda kc: ht[:, kc, 0:n], [htb], n)
                fw.op("act", lambda h, psG=psG: h.activation(out=sig[:, 0:n], in_=psG[:, 0:n], func=AF.Sigmoid),
                      reads=[pbG], writes=[sigb])
                fw.op("dve", lambda h, psA=psA: h.tensor_tensor(out=mtmp[:, 0:n], in0=psA[:, 0:n], in1=sig[:, 0:n], op=ALU.mult),
                      reads=[pbA, sigb], writes=[mtmpb])
                wB, wBb = wload(wrot2, w_brsb[l][:, ms], 8)
                wG2, wG2b = wload(wrot2, w_gate[l][:, 1024 + m * 128:1024 + (m + 1) * 128], 8)
                psB, pbB = psum()
                mm(psB, pbB, wB, wBb, 8, lambda kc: oT[:, kc, 0:n], [oTb], n)
                psG2, pbG2 = psum()
                mm(psG2, pbG2, wG2, wG2b, 8, lambda kc: ht[:, kc, 0:n], [htb], n)
                fw.op("act", lambda h, psG2=psG2: h.activation(out=sig[:, 0:n], in_=psG2[:, 0:n], func=AF.Sigmoid),
                      reads=[pbG2], writes=[sigb])
                fw.op("dve", lambda h, psB=psB: h.tensor_tensor(out=sig[:, 0:n], in0=psB[:, 0:n], in1=sig[:, 0:n], op=ALU.mult),
                      reads=[pbB, sigb], writes=[sigb])
                fw.op("dve", lambda h, m=m: h.tensor_tensor(out=mer[:, m, 0:n], in0=sig[:, 0:n], in1=mtmp[:, 0:n], op=ALU.add),
                      reads=[sigb, mtmpb], writes=[merb])
            for m in range(8):
                wO, wOb = wload(wrot2, w_out[l][:, m * 128:(m + 1) * 128], 8)
                ps, pb = psum()
                mm(ps, pb, wO, wOb, 8, lambda kc: mer[:, kc, 0:n], [merb], n)
                evac_norm_add(ps, pb, m, n)
            post_norm_add(l, 1, n)
            rs, rsb = rstd_from([xt[:, kc, 0:n] for kc in range(8)], [xtb], n, D, rots)
            for kc in range(8):
                fw.op("dve", lambda h, kc=kc, rs=rs: h.scalar_tensor_tensor(
                    out=h2T[:, kc, c0:c0 + n], in0=xt[:, kc, 0:n], scalar=gain(l, 2, kc), in1=rs[:, 0:n],
                    op0=ALU.mult, op1=ALU.mult), reads=[xtb, rsb, CB], writes=[h2Tb])
            fw.dma("sp", xscv[:, :, c0:c0 + n], xt[:, :, 0:n], reads=[xtb], writes=[DRAMB["xscr"]])
        if STAGE < 5:
            return
        tailv = h2T[:, :, 0:2048].rearrange("p k (a b) -> p k a b", a=4)[:, :, :, 510:512]
        fw.op("dve", lambda h: h.tensor_copy(out=tl16[:, :, :].rearrange("p k (a b) -> p k a b", a=4), in_=tailv),
              reads=[h2Tb], writes=[tl16b])
        for f in range(32):
            wU, wUb = wload(wrot2, w_up[l][:, f * 128:(f + 1) * 128], 8)
            ps, pb = psum()
            for kc in range(8):
                fw.op("pe", lambda h, kc=kc, ps=ps, wU=wU: h.matmul(
                    ps[:, 0:8], lhsT=wU[:, kc, :], rhs=tl16[:, kc, :],
                    start=(kc == 0), stop=(kc == 7)), reads=[wUb, tl16b], writes=[pb])
            fw.op("act", lambda h, f=f, ps=ps: h.activation(out=gtl[:, f, 1:5, :], in_=ps[:, 0:8].rearrange("p (a b) -> p a b", a=4),
                                                            func=AF.Copy), reads=[pb], writes=[gtlb])
        fw.dma("sp", o_ffnP[l], gtl[:, :, 4, :], reads=[gtlb], writes=[DRAMB["outs"]])
        fw.dma("sp", halo_in.rearrange("p (f t) -> p f t", f=32), gtl[:, :, 4, :], reads=[gtlb], writes=[DRAMB["halo_in"]])
        allgather(halo_in, halo_all, DRAMB["halo_in"], DRAMB["halo_all"])
        fw.dma("sp", halo_buf[128:, :], halo_all[:, :], reads=[DRAMB["halo_all"]], writes=[DRAMB["halo_buf"]])
        fw.dma("sp", gtl[:, :, 0, :], halo_buf[bass.ds(rank_sp * 128, 128), :].rearrange("p (f t) -> p f t", f=32),
               reads=[DRAMB["halo_buf"]], writes=[gtlb])
        fw.barrier()
        AR.off = mark
        ff, ffb = AR.alloc([128, 32, 512], BF16, "ff")
        gpad, gpadb = AR.alloc([128, 4 * 34 + 512], F32, "gpad")
        cva, cvab = AR.alloc([128, 512], F32, "cva")
        gl, glb = AR.alloc([128, 512], F32, "gl")
        xb16, xb16b = AR.alloc([128, 8, 512], BF16, "xb16")
        pt16, pt16b = AR.alloc([128, 2, 512], BF16, "pt16")
        ht, htb = AR.alloc([128, 8, 512], BF16, "ht2")
        fst, fstb = AR.alloc([128, 32, 4, 2], F32, "fst")
        for ti, (c0, n) in enumerate(TILES):
            samp = ti == 4
            fw.dma("sp", xt[:, :, 0:n], xscv[:, :, c0:c0 + n], reads=[DRAMB["xscr"]], writes=[xtb])
            fw.dma("pool", pt16[:, :, 0:n], pT[l].rearrange("(k p) t -> p k t", p=128)[:, :, c0:c0 + n], writes=[pt16b])
            nseq, L = (4, 32) if samp else (1, 512)
            gpv = gpad[:, 0:nseq * (L + 2)].rearrange("p (s t) -> p s t", s=nseq)
            for f in range(32):
                wU, wUb = wload(wrot2, w_up[l][:, f * 128:(f + 1) * 128], 8)
                wV, wVb = wload(wrot2, w_up[l][:, 4096 + f * 128:4096 + (f + 1) * 128], 8)
                psg, pbg = psum()
                mm(psg, pbg, wU, wUb, 8, lambda kc: h2T[:, kc, c0:c0 + n], [h2Tb], n)
                psu, pbu = psum()
                mm(psu, pbu, wV, wVb, 8, lambda kc: h2T[:, kc, c0:c0 + n], [h2Tb], n)
                if samp:
                    fw.dma("sp", gpv[:, :, 0:2], ffn0[l][:, f, :, :], writes=[gpadb])
                else:
                    fw.op("pool", lambda h, f=f, ti=ti: h.tensor_copy(out=gpv[:, 0, 0:2], in_=gtl[:, f, ti, :]),
                          reads=[gtlb], writes=[gpadb])
                fw.op("act", lambda h, psg=psg: h.activation(out=gpv[:, :, 2:2 + L], in_=psg[:, 0:n].rearrange("p (s t) -> p s t", s=nseq),
                                                             func=AF.Copy), reads=[pbg], writes=[gpadb])
                cvv = cva[:, 0:n].rearrange("p (s t) -> p s t", s=nseq)
                fw.op("dve", lambda h, f=f, cvv=cvv: h.tensor_scalar(
                    out=cvv, in0=gpv[:, :, 0:L], scalar1=fcw_t[:, l, f * 3:f * 3 + 1], scalar2=fcb_t[:, l, f:f + 1],
                    op0=ALU.mult, op1=ALU.add), reads=[gpadb, CB], writes=[cvab])
                for tap in (1, 2):
                    fw.op("dve", lambda h, f=f, tap=tap, cvv=cvv: h.scalar_tensor_tensor(
                        out=cvv, in0=gpv[:, :, tap:tap + L], scalar=fcw_t[:, l, f * 3 + tap:f * 3 + tap + 1], in1=cvv,
                        op0=ALU.mult, op1=ALU.add), reads=[gpadb, cvab, CB], writes=[cvab])
                fw.op("act", lambda h: h.activation(out=gl[:, 0:n], in_=cva[:, 0:n], func=AF.Gelu_apprx_tanh),
                      reads=[cvab], writes=[glb])
                fw.op("dve", lambda h, f=f, psu=psu: h.tensor_tensor(out=ff[:, f, 0:n], in0=gl[:, 0:n], in1=psu[:, 0:n], op=ALU.mult),
                      reads=[glb, pbu], writes=[ffb])
                if samp:
                    fw.op("pool", lambda h, f=f: h.tensor_copy(out=fst[:, f, :, :], in_=gpv[:, :, L:L + 2]),
                          reads=[gpadb], writes=[fstb])
            if samp:
                fw.dma("sp", o_ffnS[l], fst, reads=[fstb], writes=[DRAMB["outs"]])
            for m in range(8):
                wD, wDb = wload(wrot, w_down[l][:, m * 128:(m + 1) * 128], 32)
                ps, pb = psum()
                mm(ps, pb, wD, wDb, 32, lambda kc: ff[:, kc, 0:n], [ffb], n)
                evac_norm_add(ps, pb, m, n)
            post_norm_add(l, 3, n)
            for kc in range(8):
                fw.op("act", lambda h, kc=kc: h.activation(out=xb16[:, kc, 0:n], in_=xt[:, kc, 0:n], func=AF.Copy),
                      reads=[xtb], writes=[xb16b])
            for m in range(8):
                wP, wPb = wload(wrot2, w_ple[l][:, m * 128:(m + 1) * 128], 2)
                wQ, wQb = wload(wrot2, w_pleg[l][:, m * 128:(m + 1) * 128], 8)
                psP, pbP = psum()
                mm(psP, pbP, wP, wPb, 2, lambda kc: pt16[:, kc, 0:n], [pt16b], n)
                psQ, pbQ = psum()
                mm(psQ, pbQ, wQ, wQb, 8, lambda kc: xb16[:, kc, 0:n], [xb16b], n)
                fw.op("act", lambda h, psQ=psQ: h.activation(out=gl[:, 0:n], in_=psQ[:, 0:n], func=AF.Sigmoid),
                      reads=[pbQ], writes=[glb])
                fw.op("dve", lambda h, m=m, psP=psP: h.tensor_tensor(out=stg[:, m, 0:n], in0=psP[:, 0:n], in1=gl[:, 0:n], op=ALU.mult),
                      reads=[pbP, glb], writes=[stgb])
            post_norm_add(l, 4, n)
            fw.dma("sp", xdv[:, :, c0:c0 + n], xt[:, :, 0:n], reads=[xtb], writes=[DRAMB["xscr" if l == 0 else "outs"]])
            if l == 0:
                d1_tile(1, xt, xtb, c0, n, rots, (ht, htb))

    AR.reset()
    z64, z64b = AR.alloc([128, 64], F32, "z64")
    fw.op("dve", lambda h: h.memset(z64, 0.0), writes=[z64b])
    fw.dma("sp", halo_buf[0:128, :], z64, reads=[z64b], writes=[DRAMB["halo_buf"]])
    fw.barrier()
    phase_d1_l0()
    for l in range(2):
        allgather(hT_in, hT_all, DRAMB["hT_in"], DRAMB["hT_all"])
        if STAGE >= 2:
            phase_m(l)
        if STAGE >= 4:
            allgather(mix_in, mix_all, DRAMB["mix_in"], DRAMB["mix_all"])
            phase_d2(l)
        if STAGE < 6:
            break
    fw.emit(st)
    st.close()
    return nc


def _consts():
    i = np.arange(128)
    same = (i[:, None] // 32) == (i[None, :] // 32)
    cm = np.zeros((128, 10, 128), np.float32)
    cm[:, 0] = (i[:, None] <= i[None, :])
    cm[:, 1] = (i[:, None] <= i[None, :]) & same
    cm[:, 2] = 1.0
    cm[:, 3] = same
    for j in range(4):
        cm[32 * j, 4 + j, :] = 1.0
    cm[:, 8] = (i[:, None] > i[None, :])
    cm[:, 9] = np.eye(128)
    seqm = np.zeros((128, 4), np.float32)
    for j in range(4):
        seqm[32 * j:32 * j + 32, j] = 1.0
    q = np.arange(512)
    am = np.zeros((128, 5, 512), np.float32)
    for b in range(4):
        am[:, b] = ((128 * b + i)[:, None] < q[None, :])
    qq = q % 128
    am[:, 4] = (i[:, None] < qq[None, :]) & ((i[:, None] // 32) == (qq[None, :] // 32))
    return cm, seqm, am


def _prep(inp):
    f = lambda a: np.ascontiguousarray(a, dtype=np.float32)
    cm, seqm, am = _consts()
    w_in = inp["w_in"]
    gfm = lambda v, nch: v.reshape(2, nch, 128).transpose(0, 2, 1)
    gains = np.stack([gfm(inp[k], 8) for k in ("norm_pre_mix", "norm_post_mix", "norm_pre_ffn", "norm_post_ffn", "norm_ple")], axis=2)
    gssd = inp["ssd_norm"].reshape(2, 8, 2, 128).transpose(0, 3, 2, 1)
    fcw = inp["ffn_conv_w"].reshape(2, 3, 32, 128).transpose(0, 3, 2, 1)
    fcb = inp["ffn_conv_b"].reshape(2, 32, 128).transpose(0, 2, 1)
    shared = dict(
        w_gate=f(w_in[:, :, 8224:10272]), w_brssd=f(inp["w_br_ssd"]), w_brsb=f(inp["w_br_sb"]), w_out=f(inp["w_out"]),
        w_up=f(inp["w_up"]), w_down=f(inp["w_down"]), w_ple=f(inp["w_ple"]), w_pleg=f(inp["w_ple_gate"]),
        gains=f(gains), gssd=f(gssd), fcw=f(fcw), fcb=f(fcb), cmask=cm, seqm=seqm, amask=am)
    maps = []
    for c in range(NC):
        g = c // 2
        m = dict(shared)
        xo = np.concatenate([inp["x_prompt"][0, 2048 * c:2048 * (c + 1)], inp["x_sample"][4 * c:4 * c + 4].reshape(128, D)], 0)
        m["xT"] = f(xo.T)
        po = np.concatenate([inp["p_prompt"][:, 0, 2048 * c:2048 * (c + 1)], inp["p_sample"][:, 4 * c:4 * c + 4].reshape(2, 128, 256)], 1)
        m["pT"] = f(po.transpose(0, 2, 1))
        cols = np.concatenate([5152 + 128 * c + np.arange(128), 6176 + 128 * c + np.arange(128),
                               256 * c + np.arange(256), 2048 + 256 * c + np.arange(256),
                               4096 + 128 * g + np.arange(128), 4608 + 128 * g + np.arange(128)])
        m["w_fm"] = f(w_in[:, :, cols])
        m["w_tm"] = f(w_in[:, :, np.concatenate([7200 + 128 * c + np.arange(128), 5120 + 4 * c + np.arange(4)])])
        ch = np.concatenate([256 * c + np.arange(256), 2048 + 128 * g + np.arange(128), 2560 + 128 * g + np.arange(128)])
        m["convw"] = f(inp["ssd_conv_w"][:, :, ch].reshape(2, 4, 4, 128).transpose(0, 3, 2, 1))
        m["convb"] = f(inp["ssd_conv_b"][:, ch].reshape(2, 4, 128).transpose(0, 2, 1))
        m["dtb"] = f(np.broadcast_to(inp["ssd_dt_bias"][:, None, 4 * c:4 * c + 4], (2, 128, 4)))
        m["alog"] = f(np.broadcast_to(inp["ssd_a_log"][:, None, 4 * c:4 * c + 4], (2, 128, 4)))
        m["dskip"] = f(np.repeat(inp["ssd_d"][:, 4 * c:4 * c + 4], 64, axis=1).reshape(2, 2, 128).transpose(0, 2, 1))
        m["conv0"] = f(inp["state_ssd_conv"][:, :, :, ch].reshape(2, 32, 3, 4, 128).transpose(0, 4, 3, 1, 2))
        hs = inp["state_ssd"][:, :, 4 * c:4 * c + 4]
        m["h0p"] = f(hs.reshape(2, 32, 2, 128, 128))
        m["h0T"] = f(hs.transpose(0, 1, 4, 2, 3).reshape(2, 32, 128, 256))
        kc_ = inp["cache_sb_k"][:, :, :, c, :].reshape(2, 2, 16, 16, 128, 128)
        m["kcT"] = f(kc_.transpose(0, 1, 3, 5, 2, 4))
        vc_ = inp["cache_sb_v"][:, :, :, c, :].reshape(2, 2, 16, 16, 128, 128)
        m["vc"] = f(vc_.transpose(0, 1, 3, 4, 2, 5))
        m["ffn0"] = f(inp["state_ffn_conv"][:, 4 * c:4 * c + 4].reshape(2, 4, 2, 32, 128).transpose(0, 4, 3, 1, 2))
        maps.append(m)
    return maps


_NC_CACHE = {}


def kernel(**inputs):
    inp = {k: np.asarray(v) for k, v in inputs.items()}
    maps = _prep(inp)
    if "nc" not in _NC_CACHE:
        _NC_CACHE["nc"] = build()
    nc = _NC_CACHE["nc"]
    res = run_bass_kernel_spmd(nc, maps, core_ids=list(range(NC)))
    R = res.results
    y_prompt = np.zeros((1, 16384, D), np.float32)
    y_sample = np.zeros((32, 32, D), np.float32)
    p_conv = np.zeros((2, 1, 3, 3072), np.float32)
    p_ssd = np.zeros((2, 1, 32, 64, 128), np.float32)
    p_k = np.zeros((2, 1, 16384, 8, 128), np.float32)
    p_v = np.zeros((2, 1, 16384, 8, 128), np.float32)
    p_ffn = np.zeros((2, 1, 2, 4096), np.float32)
    s_conv = np.zeros((2, 32, 3, 3072), np.float32)
    s_ssd = np.zeros((2, 32, 32, 64, 128), np.float32)
    s_k = np.zeros((2, 32, 32, 8, 128), np.float32)
    s_v = np.zeros((2, 32, 32, 8, 128), np.float32)
    s_ffn = np.zeros((2, 32, 2, 4096), np.float32)
    for c in range(NC):
        r = R[c]
        g = c // 2
        yt = r["yT"].T
        y_prompt[0, 2048 * c:2048 * (c + 1)] = yt[:2048]
        y_sample[4 * c:4 * c + 4] = yt[2048:].reshape(4, 32, D)
        ch = np.concatenate([256 * c + np.arange(256), 2048 + 128 * g + np.arange(128), 2560 + 128 * g + np.arange(128)])
        p_conv[:, 0][:, :, ch] = r["o_convP"].transpose(0, 3, 2, 1).reshape(2, 3, 512)
        s_conv[:, :, :, ch] = r["o_convS"].transpose(0, 3, 4, 2, 1).reshape(2, 32, 3, 512)
        p_ssd[:, 0, 4 * c:4 * c + 4] = r["o_ssdP"].reshape(2, 128, 4, 64).transpose(0, 2, 3, 1)
        s_ssd[:, :, 4 * c:4 * c + 4] = r["o_ssdS"].reshape(2, 32, 4, 64, 128)
        kt = r["o_kT"]
        p_k[:, 0, :, c, :] = kt[:, :, :16384].transpose(0, 2, 1)
        s_k[:, :, :, c, :] = kt[:, :, 16384:].transpose(0, 2, 1).reshape(2, 32, 32, 128)
        ov = r["o_v"]
        p_v[:, 0, :, c, :] = ov[:, :16384]
        s_v[:, :, :, c, :] = ov[:, 16384:].reshape(2, 32, 32, 128)
        if c == NC - 1:
            p_ffn[:, 0] = r["o_ffnP"].transpose(0, 3, 2, 1).reshape(2, 2, 4096)
        s_ffn[:, 4 * c:4 * c + 4] = r["o_ffnS"].transpose(0, 3, 4, 2, 1).reshape(2, 4, 2, 4096)
    return (y_prompt, y_sample, p_conv, p_ssd, p_k, p_v, p_ffn, s_conv, s_ssd, s_k, s_v, s_ffn)
```
